# Optimizing a Trainium2 kernel written in Bass

```python
import math
import jax, jax.numpy as jnp
from jax import lax
import numpy as np

D_MODEL = 2048
BATCH = 2
SEQ = 8192
DEPTH = 4
DEC_BATCH = 8
DEC_SEQ = 64
PAST_LEN = 2048

CHUNK = 64
N_PAST_CHUNKS = 8
BAND_PAST = CHUNK * N_PAST_CHUNKS
BAND = BAND_PAST + CHUNK
N_HEADS = 16
HEAD_DIM = D_MODEL // N_HEADS
MAX_REL = 128
N_REL = 2 * MAX_REL + 1
SSM_GROUP = 16
N_GROUPS = D_MODEL // SSM_GROUP
SSM_STATE = 64
D_FF = 4 * D_MODEL
N_MIXERS = 2
N_ATTN = (DEPTH + 1) // 2
N_SSM = DEPTH // 2
RMS_EPS = 1e-5
NEG_INF = -1e30
DT_MIN = 0.001
DT_MAX = 0.1

kernel_name = "chunk_band_attn_s5_hybrid_stream_step"


def rms_norm(x, g):
    x32 = x.astype(jnp.float32)
    y = x32 * lax.rsqrt(jnp.mean(x32 * x32, axis=-1, keepdims=True) + RMS_EPS)
    return (y * g.astype(jnp.float32)).astype(x.dtype)


def sq_relu_mlp(x, w_up, w_down):
    h = jax.nn.relu(x @ w_up)
    return ((h * h) @ w_down).astype(x.dtype)


def band_attend(q, k, v, q_pos, k_pos, rel_bias):
    s = jnp.einsum('bqhd,bkhd->bhqk', q, k).astype(jnp.float32) * (HEAD_DIM ** -0.5)
    rel = jnp.clip(q_pos[:, None] - k_pos[None, :], -MAX_REL, MAX_REL) + MAX_REL
    s = s + rel_bias.astype(jnp.float32)[:, rel][None]
    q_chunk = q_pos // CHUNK
    k_chunk = k_pos // CHUNK
    mask = ((k_pos[None, :] >= 0)
            & (k_chunk[None, :] <= q_chunk[:, None])
            & (k_chunk[None, :] >= q_chunk[:, None] - N_PAST_CHUNKS))
    s = jnp.where(mask[None, None], s, NEG_INF)
    p = jax.nn.softmax(s, axis=-1).astype(v.dtype)
    return jnp.einsum('bhqk,bkhd->bqhd', p, v)


def attn_qkv(h, w_qkv):
    b, t, _ = h.shape
    q, k, v = jnp.split(h @ w_qkv, 3, axis=-1)
    shp = (b, t, N_HEADS, HEAD_DIM)
    return q.reshape(shp), k.reshape(shp), v.reshape(shp)


def attn_prompt(h, w_qkv, w_o, rel_bias):
    b, s, _ = h.shape
    q, k, v = attn_qkv(h, w_qkv)
    nc = s // CHUNK
    zpad = jnp.zeros((b, BAND_PAST, N_HEADS, HEAD_DIM), k.dtype)
    k_pad = jnp.concatenate([zpad, k], axis=1)
    v_pad = jnp.concatenate([zpad, v], axis=1)
    q_blocks = q.reshape(b, nc, CHUNK, N_HEADS, HEAD_DIM).swapaxes(0, 1)

    def one_chunk(args):
        c, q_blk = args
        start = c * CHUNK
        k_band = lax.dynamic_slice_in_dim(k_pad, start, BAND, axis=1)
        v_band = lax.dynamic_slice_in_dim(v_pad, start, BAND, axis=1)
        q_pos = start + jnp.arange(CHUNK, dtype=jnp.int32)
        k_pos = start - BAND_PAST + jnp.arange(BAND, dtype=jnp.int32)
        return band_attend(q_blk, k_band, v_band, q_pos, k_pos, rel_bias)

    o = lax.map(one_chunk, (jnp.arange(nc, dtype=jnp.int32), q_blocks))
    o = o.swapaxes(0, 1).reshape(b, s, D_MODEL)
    keep = min(BAND_PAST, s)
    return (o @ w_o).astype(h.dtype), k[:, s - keep:], v[:, s - keep:]


def attn_sample(h, k_cache, v_cache, w_qkv, w_o, rel_bias):
    b, t, _ = h.shape
    q, k, v = attn_qkv(h, w_qkv)
    l_c = k_cache.shape[1]
    k_band = jnp.concatenate([k_cache.astype(k.dtype), k], axis=1)
    v_band = jnp.concatenate([v_cache.astype(v.dtype), v], axis=1)
    q_pos = PAST_LEN + jnp.arange(t, dtype=jnp.int32)
    k_pos = PAST_LEN - l_c + jnp.arange(l_c + t, dtype=jnp.int32)
    o = band_attend(q, k_band, v_band, q_pos, k_pos, rel_bias).reshape(b, t, D_MODEL)
    return (o @ w_o).astype(h.dtype), k, v


def s5_discretise(a_re, a_im, log_dt, b_re, b_im):
    a_re = a_re.astype(jnp.float32)
    a_im = a_im.astype(jnp.float32)
    b_re = b_re.astype(jnp.float32)
    b_im = b_im.astype(jnp.float32)
    dt = jnp.exp(log_dt.astype(jnp.float32))[:, None]
    mag = jnp.exp(dt * a_re)
    abar_re = mag * jnp.cos(dt * a_im)
    abar_im = mag * jnp.sin(dt * a_im)
    den = a_re * a_re + a_im * a_im
    num_re = abar_re - 1.0
    num_im = abar_im
    z_re = ((num_re * a_re + num_im * a_im) / den)[..., None]
    z_im = ((num_im * a_re - num_re * a_im) / den)[..., None]
    bbar_re = z_re * b_re - z_im * b_im
    bbar_im = z_re * b_im + z_im * b_re
    return abar_re, abar_im, bbar_re, bbar_im


def _ssm_combine(e1, e2):
    a1r, a1i, b1r, b1i = e1
    a2r, a2i, b2r, b2i = e2
    ar = a1r * a2r - a1i * a2i
    ai = a1r * a2i + a1i * a2r
    br = a2r * b1r - a2i * b1i + b2r
    bi = a2r * b1i + a2i * b1r + b2i
    return ar, ai, br, bi


def s5_block(u, x0_re, x0_im, abar_re, abar_im, bbar_re, bbar_im, c_re, c_im, d):
    b, t, _ = u.shape
    u32 = u.astype(jnp.float32)
    ug = u32.reshape(b, t, N_GROUPS, SSM_GROUP)
    bu_re = jnp.einsum('btgs,gps->btgp', ug, bbar_re)
    bu_im = jnp.einsum('btgs,gps->btgp', ug, bbar_im)
    a_r = jnp.broadcast_to(abar_re, bu_re.shape)
    a_i = jnp.broadcast_to(abar_im, bu_re.shape)
    acc_re, acc_im, h_re, h_im = lax.associative_scan(_ssm_combine, (a_r, a_i, bu_re, bu_im), axis=1)
    x0r = x0_re[:, None]
    x0i = x0_im[:, None]
    h_re = h_re + acc_re * x0r - acc_im * x0i
    h_im = h_im + acc_re * x0i + acc_im * x0r
    y = (jnp.einsum('btgp,gsp->btgs', h_re, c_re.astype(jnp.float32))
         - jnp.einsum('btgp,gsp->btgs', h_im, c_im.astype(jnp.float32)))
    y = y.reshape(b, t, D_MODEL) + d.astype(jnp.float32) * u32
    return y, h_re[:, -1], h_im[:, -1]


def s5_glu(y, w_a, w_b, dtype):
    g = jax.nn.gelu(y).astype(dtype)
    return ((g @ w_a) * jax.nn.sigmoid(g @ w_b)).astype(dtype)


def ssm_prompt(h, disc, c_re, c_im, d, w_a, w_b):
    b, s, _ = h.shape
    nc = s // CHUNK
    u_blocks = h.reshape(b, nc, CHUNK, D_MODEL).swapaxes(0, 1)
    x0 = jnp.zeros((b, N_GROUPS, SSM_STATE), jnp.float32)

    def step(carry, u_blk):
        xr, xi = carry
        y, xr, xi = s5_block(u_blk, xr, xi, *disc, c_re, c_im, d)
        return (xr, xi), y

    (xr, xi), ys = lax.scan(step, (x0, x0), u_blocks)
    y = ys.swapaxes(0, 1).reshape(b, s, D_MODEL)
    return s5_glu(y, w_a, w_b, h.dtype), xr, xi


def ssm_sample(h, st_re, st_im, disc, c_re, c_im, d, w_a, w_b):
    y, xr, xi = s5_block(h, st_re.astype(jnp.float32), st_im.astype(jnp.float32), *disc, c_re, c_im, d)
    return s5_glu(y, w_a, w_b, h.dtype), xr, xi


def setup_inputs(seed: int = 0) -> dict:
    key = jax.random.key(seed)
    ks = jax.random.split(key, 26)
    f32 = jnp.float32
    cache_len = min(BAND_PAST, PAST_LEN)
    nrm = lambda k, shp, sc: jax.random.normal(k, shp, f32) * sc
    a_im_base = (math.pi * jnp.arange(SSM_STATE, dtype=f32))[None, None, :]
    return {
        "x_prompt": nrm(ks[0], (BATCH, SEQ, D_MODEL), 1.0),
        "x_sample": nrm(ks[1], (DEC_BATCH, DEC_SEQ, D_MODEL), 1.0),
        "cache_attn_k": nrm(ks[2], (N_ATTN, DEC_BATCH, cache_len, N_HEADS, HEAD_DIM), 1.0),
        "cache_attn_v": nrm(ks[3], (N_ATTN, DEC_BATCH, cache_len, N_HEADS, HEAD_DIM), 1.0),
        "state_ssm_re": nrm(ks[4], (N_SSM, DEC_BATCH, N_GROUPS, SSM_STATE), 0.5),
        "state_ssm_im": nrm(ks[5], (N_SSM, DEC_BATCH, N_GROUPS, SSM_STATE), 0.5),
        "norm_mix": 1.0 + nrm(ks[6], (DEPTH, D_MODEL), 0.02),
        "norm_mlp": 1.0 + nrm(ks[7], (DEPTH, D_MODEL), 0.02),
        "norm_final": 1.0 + nrm(ks[8], (D_MODEL,), 0.02),
        "attn_w_qkv": nrm(ks[9], (N_ATTN, D_MODEL, 3 * D_MODEL), D_MODEL ** -0.5),
        "attn_w_o": nrm(ks[10], (N_ATTN, D_MODEL, D_MODEL), D_MODEL ** -0.5),
        "attn_rel_bias": nrm(ks[11], (N_ATTN, N_HEADS, N_REL), 0.1),
        "ssm_a_re": -0.5 + nrm(ks[12], (N_SSM, N_GROUPS, SSM_STATE), 0.01),
        "ssm_a_im": a_im_base + nrm(ks[13], (N_SSM, N_GROUPS, SSM_STATE), 0.01),
        "ssm_log_dt": jax.random.uniform(ks[14], (N_SSM, N_GROUPS), f32, math.log(DT_MIN), math.log(DT_MAX)),
        "ssm_b_re": nrm(ks[15], (N_SSM, N_GROUPS, SSM_STATE, SSM_GROUP), (2 * SSM_GROUP) ** -0.5),
        "ssm_b_im": nrm(ks[16], (N_SSM, N_GROUPS, SSM_STATE, SSM_GROUP), (2 * SSM_GROUP) ** -0.5),
        "ssm_c_re": nrm(ks[17], (N_SSM, N_GROUPS, SSM_GROUP, SSM_STATE), SSM_STATE ** -0.5),
        "ssm_c_im": nrm(ks[18], (N_SSM, N_GROUPS, SSM_GROUP, SSM_STATE), SSM_STATE ** -0.5),
        "ssm_d": nrm(ks[19], (N_SSM, D_MODEL), 1.0),
        "ssm_w_glu_a": nrm(ks[20], (N_SSM, D_MODEL, D_MODEL), D_MODEL ** -0.5),
        "ssm_w_glu_b": nrm(ks[21], (N_SSM, D_MODEL, D_MODEL), D_MODEL ** -0.5),
        "mlp_w_up": nrm(ks[22], (DEPTH, D_MODEL, D_FF), D_MODEL ** -0.5),
        "mlp_w_down": nrm(ks[23], (DEPTH, D_FF, D_MODEL), D_FF ** -0.5),
    }


def reference(x_prompt, x_sample, cache_attn_k, cache_attn_v, state_ssm_re, state_ssm_im,
              norm_mix, norm_mlp, norm_final, attn_w_qkv, attn_w_o, attn_rel_bias,
              ssm_a_re, ssm_a_im, ssm_log_dt, ssm_b_re, ssm_b_im, ssm_c_re, ssm_c_im, ssm_d,
              ssm_w_glu_a, ssm_w_glu_b, mlp_w_up, mlp_w_down):
    xp = x_prompt
    xs = x_sample
    k_p, v_p, k_s, v_s = [], [], [], []
    sr_p, si_p, sr_s, si_s = [], [], [], []
    for i in range(DEPTH):
        j = i // N_MIXERS
        hp = rms_norm(xp, norm_mix[i])
        hs = rms_norm(xs, norm_mix[i])
        if i % N_MIXERS == 0:
            op, kp, vp = attn_prompt(hp, attn_w_qkv[j], attn_w_o[j], attn_rel_bias[j])
            os_, kn, vn = attn_sample(hs, cache_attn_k[j], cache_attn_v[j],
                                      attn_w_qkv[j], attn_w_o[j], attn_rel_bias[j])
            k_p.append(kp); v_p.append(vp); k_s.append(kn); v_s.append(vn)
        else:
            disc = s5_discretise(ssm_a_re[j], ssm_a_im[j], ssm_log_dt[j], ssm_b_re[j], ssm_b_im[j])
            op, rp, ip = ssm_prompt(hp, disc, ssm_c_re[j], ssm_c_im[j], ssm_d[j],
                                    ssm_w_glu_a[j], ssm_w_glu_b[j])
            os_, rn, inn = ssm_sample(hs, state_ssm_re[j], state_ssm_im[j], disc, ssm_c_re[j],
                                      ssm_c_im[j], ssm_d[j], ssm_w_glu_a[j], ssm_w_glu_b[j])
            sr_p.append(rp); si_p.append(ip); sr_s.append(rn); si_s.append(inn)
        xp = xp + op
        xs = xs + os_
        xp = xp + sq_relu_mlp(rms_norm(xp, norm_mlp[i]), mlp_w_up[i], mlp_w_down[i])
        xs = xs + sq_relu_mlp(rms_norm(xs, norm_mlp[i]), mlp_w_up[i], mlp_w_down[i])
    y_prompt = rms_norm(xp, norm_final)
    y_sample = rms_norm(xs, norm_final)
    new_k_prompt = jnp.stack(k_p)
    new_v_prompt = jnp.stack(v_p)
    new_ssm_re_prompt = jnp.stack(sr_p)
    new_ssm_im_prompt = jnp.stack(si_p)
    new_k_sample = jnp.stack(k_s)
    new_v_sample = jnp.stack(v_s)
    new_ssm_re_sample = jnp.stack(sr_s)
    new_ssm_im_sample = jnp.stack(si_s)
    return (y_prompt, y_sample, new_k_prompt, new_v_prompt, new_ssm_re_prompt, new_ssm_im_prompt,
            new_k_sample, new_v_sample, new_ssm_re_sample, new_ssm_im_sample)
```

```python
import numpy as np
from contextlib import ExitStack
import concourse.bass as bass
import concourse.mybir as mybir
from concourse.bass_utils import run_bass_kernel_spmd

F32 = mybir.dt.float32
BF16 = mybir.dt.bfloat16
AF = mybir.ActivationFunctionType
ALU = mybir.AluOpType

D = 2048
NT = 2112
NP = 2048
FT = 16
TT = [(0, 512), (512, 512), (1024, 512), (1536, 512), (2048, 64)]
DFF = 8192
EPS = 1e-5
NCORES = 8


ALLBUFS = []
NPS_USE = 6


class Buf:
    __slots__ = ("name", "last_w", "readers", "dsem", "dcount")

    def __init__(self, name):
        self.name = name
        self.last_w = None
        self.readers = {}
        self.dsem = None
        self.dcount = 0
        ALLBUFS.append(self)


class Eng:
    def __init__(self, eng, sem):
        self.eng = eng
        self.sem = sem
        self.count = 0
        self.known = {}


class K:
    def __init__(self, nc, es):
        self.nc = nc
        self.es = es
        self.E = {}
        for nm, eng in (("pe", nc.tensor), ("act", nc.scalar), ("dve", nc.vector),
                        ("pool", nc.gpsimd), ("sp", nc.sync)):
            self.E[nm] = Eng(eng, es.enter_context(nc.semaphore("prog_" + nm)))
        self.bar_sem = es.enter_context(nc.semaphore("bar"))
        self.bar_count = 0
        self.dma_bufs = []
        self.nsem = 6
        self.psum = []
        self.ps_i = 0

    def _wait(self, E, tok):
        sem, val = tok
        key = id(sem)
        if E.known.get(key, 0) >= val:
            return
        E.eng.wait_ge(sem, val)
        E.known[key] = val

    def _deps(self, reads, writes):
        deps = []
        for b in reads:
            if b.last_w is not None:
                deps.append(b.last_w)
        for b in writes:
            if b.last_w is not None:
                deps.append(b.last_w)
            deps.extend(b.readers.values())
        return deps

    def _commit(self, tok, reads, writes):
        for b in reads:
            b.readers[id(tok[0])] = tok
        for b in writes:
            b.last_w = tok
            b.readers = {}

    def op(self, e, fn, reads=(), writes=(), inc=True):
        E = self.E[e]
        for d in self._deps(reads, writes):
            if e == "pe" and d[0] is E.sem:
                continue
            self._wait(E, d)
        ins = fn(E.eng)
        if inc:
            E.count += 1
            ins.then_inc(E.sem, 1)
            tok = (E.sem, E.count)
        else:
            tok = (E.sem, E.count + 1)
        self._commit(tok, reads, writes)
        return tok

    def dma(self, q, out, in_, reads, writes, **kw):
        E = self.E[q]
        out_dram = "DRam" in type(out.tensor).__name__
        in_dram = "DRam" in type(in_.tensor).__name__
        owner = writes[0] if (not out_dram or in_dram) else reads[0]
        if out_dram and not in_dram:
            writes = []
        elif in_dram and not out_dram:
            reads = []
        sw = (q == "pool")
        if owner.dsem is None:
            owner.dsem = {}
            owner.dcount = {}
            self.dma_bufs.append(owner)
        if sw not in owner.dsem:
            owner.dsem[sw] = self.es.enter_context(self.nc.semaphore("d%d_%s" % (int(sw), owner.name)))
            owner.dcount[sw] = 0
            self.nsem += 1
        sem = owner.dsem[sw]
        for b in reads:
            if b.last_w is not None:
                self._wait(E, b.last_w)
        for b in writes:
            if b.last_w is not None and b.last_w[0] is not sem:
                self._wait(E, b.last_w)
            for d in b.readers.values():
                self._wait(E, d)
        ins = E.eng.dma_start(out=out, in_=in_, **kw)
        owner.dcount[sw] += 16
        ins.then_inc(sem, 16)
        tok = (sem, owner.dcount[sw])
        self._commit(tok, reads, writes)
        return tok

    def barrier(self):
        sp = self.E["sp"]
        for nm, E in self.E.items():
            if nm != "sp" and E.count > 0:
                self._wait(sp, (E.sem, E.count))
        for b in self.dma_bufs:
            for sw, sem in b.dsem.items():
                self._wait(sp, (sem, b.dcount[sw]))
        sp.eng.sem_inc(self.bar_sem, 1)
        self.bar_count += 1
        for nm, E in self.E.items():
            if nm != "sp":
                self._wait(E, (self.bar_sem, self.bar_count))
        for b in ALLBUFS:
            b.last_w = None
            b.readers = {}
        for nm, E in self.E.items():
            if E.count > 12000:
                E.sem = self.es.enter_context(self.nc.semaphore("prog_%s_%d" % (nm, self.bar_count)))
                E.count = 0
                self.nsem += 1

    def final_wait(self):
        self.barrier()

    def next_psum(self):
        p = self.psum[self.ps_i % NPS_USE]
        self.ps_i += 1
        return p


CFG = {"layers": [0, 1, 2, 3], "dbg": False, "stop": None}


class Stop(Exception):
    pass

NB = 32


def build_program():
    del ALLBUFS[:]
    nc = bass.Bass("TRN2", target_bir_lowering=False)
    es = ExitStack()
    with es:
        _build(nc, es)
    return nc


def _build(nc, es):
    k = K(nc, es)
    uid = [0]

    def dt(name, shape, dtype, kind=None):
        return nc.dram_tensor(name, shape, dtype, kind=kind) if kind else nc.dram_tensor(name, shape, dtype)

    BUFS = {}

    def B(name):
        if name not in BUFS:
            BUFS[name] = Buf(name)
        return BUFS[name]

    def alloc(scope, name, shape, dtype):
        uid[0] += 1
        return scope.enter_context(nc.sbuf_tensor("%s_%d" % (name, uid[0]), shape, dtype))

    xT_in = dt("xT", [D, NT], F32, "ExternalInput").ap()
    gam_in = dt("gam", [128, 9, FT], F32, "ExternalInput").ap()
    gath = []

    def gathered(name, L, rows, cols):
        tot = L * rows
        need = CFG.get("weights") is None or name in CFG["weights"]
        sh = dt(name + "_sh", [tot // NCORES, cols], F32, "ExternalInput" if need else None)
        bo = dt(name + "_bo", [tot // NCORES, cols], F32)
        full = dt(name + "_full", [tot, cols], F32)
        if CFG.get("weights") is None or name in CFG["weights"]:
            gath.append((name, sh, bo, full))
        return full.ap().rearrange("(l k) n -> l k n", l=L)

    wqkv = gathered("wqkv", 2, D, 3 * D)
    wo = gathered("wo", 2, D, D)
    wga = gathered("wga", 2, D, D)
    wgb = gathered("wgb", 2, D, D)
    wup = gathered("wup", 4, D, DFF)
    wdn = gathered("wdn", 4, DFF, D)
    ckT = dt("ckT", [2, D, 512], F32, "ExternalInput").ap()
    cv = dt("cv", [2, 512, D], F32, "ExternalInput").ap()
    ebraw = dt("ebraw", [2, 16, 128, 5, 128], F32, "ExternalInput").ap()
    sel8 = dt("sel8", [128, 8], F32, "ExternalInput").ap()
    hv_in = dt("hv", [128, 1], F32, "ExternalInput").ap()
    selc_in = dt("selc", [128, 24], F32, "ExternalInput").ap()
    s_are = dt("s_are", [2, 128, 64], F32, "ExternalInput").ap()
    s_aim = dt("s_aim", [2, 128, 64], F32, "ExternalInput").ap()
    s_ldt = dt("s_ldt", [2, 128, 1], F32, "ExternalInput").ap()
    s_bre = dt("s_bre", [2, 128, 1024], F32, "ExternalInput").ap()
    s_bim = dt("s_bim", [2, 128, 1024], F32, "ExternalInput").ap()
    s_cre = dt("s_cre", [2, 128, 1024], F32, "ExternalInput").ap()
    s_cim = dt("s_cim", [2, 128, 1024], F32, "ExternalInput").ap()
    s_d = dt("s_d", [2, 128, FT], F32, "ExternalInput").ap()
    st_in = dt("st_in", [2, 128, 2, 64], F32, "ExternalInput").ap()
    yT = dt("yT", [D, NT], F32, "ExternalOutput").ap()
    kT_out = dt("kT_out", [2, D, 576], F32, "ExternalOutput").ap()
    vT_out = dt("vT_out", [2, D, 576], F32, "ExternalOutput").ap()
    st_out = dt("st_out", [2, 2, 128, 2, 64], F32, "ExternalOutput").ap()
    dbg = dt("dbg", [D, NT], F32, "ExternalOutput").ap() if CFG["dbg"] else None
    XT = [dt("XT0", [D, NT], F32).ap(), dt("XT1", [D, NT], F32).ap()]
    QKV = dt("QKVs", [3, D, NT], BF16).ap()
    KVo = dt("KVo", [2, D, 576], F32).ap()
    STo = dt("STo", [2, 2, 128, 2, 64], F32).ap()
    cin = dt("cin", [2 * D, 512], BF16)
    cout = dt("cout", [NCORES * 2 * D, 512], BF16)
    cinS = dt("cinS", [2048, 128], F32)
    coutS = dt("coutS", [NCORES * 2048, 128], F32)
    tabD = dt("tabD", [4, 128, 16384], BF16).ap()
    lamD = dt("lamD", [6, 128, 64], F32).ap()
    B_XT = [B("XT0"), B("XT1")]
    B_QKV, B_cin, B_cout, B_out = B("QKV"), B("cin"), B("cout"), B("outs")
    B_tabD, B_lamD, B_cinS, B_coutS = B("tabD"), B("lamD"), B("cinS"), B("coutS")

    XN = alloc(es, "XN", [128, FT, NT], BF16)
    B_XN = [B("XN%d" % i) for i in range(FT)]
    ones = alloc(es, "ones", [128, 128], BF16)
    ident = alloc(es, "ident", [128, 128], BF16)
    gam = alloc(es, "gam_sb", [128, 9, FT], F32)
    sel_sb = alloc(es, "sel_sb", [128, 8], F32)
    hv_sb = alloc(es, "hv_sb", [128, 1], F32)
    selc = alloc(es, "selc_sb", [128, 24], F32)
    dsk = alloc(es, "dsk", [128, 2, FT], F32)
    B_const = B("const")
    for i in range(8):
        k.psum.append((es.enter_context(nc.psum_tensor("ps%d" % i, [128, 512], F32)), B("ps%d" % i)))

    k.op("pool", lambda e: e.memset(ones[:], 1.0), writes=[B_const])
    k.op("pool", lambda e: e.memset(ident[:], 0.0), writes=[B_const])
    k.op("pool", lambda e: e.affine_select(out=ident[:], in_=ident[:], pattern=[[-1, 128]],
                                           compare_op=ALU.not_equal, fill=1.0, base=0, channel_multiplier=1),
         reads=[B_const], writes=[B_const])
    k.dma("sp", gam[:], gam_in, [], [B_const])
    k.dma("sp", sel_sb[:], sel8, [], [B_const])
    k.dma("sp", hv_sb[:], hv_in, [], [B_const])
    k.dma("sp", selc[:], selc_in, [], [B_const])
    k.dma("sp", dsk[:], s_d.rearrange("l p f -> p l f"), [], [B_const])
    k.dma("sp", XT[0], xT_in, [], [B_XT[0]])
    for name, sh, bo, full in gath:
        k.dma("sp", bo.ap(), sh.ap(), [], [B(name + "_bo")])
    k.barrier()
    for name, sh, bo, full in gath:
        k.op("pool", lambda e, bo=bo, full=full: e.collective_compute(
            "AllGather", ALU.bypass, replica_groups=[list(range(NCORES))], ins=[bo.ap().opt()], outs=[full.ap().opt()]),
             reads=[B(name + "_bo")], writes=[B(name + "_full")])
    k.barrier()

    cur = [0]

    def norm_phase(gi, to_dram=None):
        with ExitStack() as sc:
            xin = [alloc(sc, "nx", [128, FT, 512], F32) for i in range(2)]
            Bx = [B("nx0"), B("nx1")]
            sq = alloc(sc, "nsq", [128, FT, 512], BF16)
            Bsq = B("nsq")
            rs = alloc(sc, "nrs", [128, 512], F32)
            Brs = B("nrs")
            src = XT[cur[0]].rearrange("(f p) t -> p f t", p=128)
            for ti, (t0, w) in enumerate(TT):
                xi, bx = xin[ti % 2], Bx[ti % 2]
                k.dma("sp", xi[:, :, 0:w], src[:, :, t0:t0 + w], [B_XT[cur[0]]], [bx])
                k.op("act", lambda e: e.activation(out=sq[:, :, 0:w], in_=xi[:, :, 0:w], func=AF.Square),
                     reads=[bx], writes=[Bsq])
                pt, bp = k.next_psum()
                for f in range(FT):
                    k.op("pe", lambda e, f=f: e.matmul(pt[:, 0:w], ones[:], sq[:, f, 0:w],
                                                       start=(f == 0), stop=(f == FT - 1)),
                         reads=[Bsq, B_const], writes=[bp], inc=(f == FT - 1))
                k.op("dve", lambda e: e.tensor_scalar(out=rs[:, 0:w], in0=pt[:, 0:w], scalar1=1.0 / D,
                                                      scalar2=EPS, op0=ALU.mult, op1=ALU.add),
                     reads=[bp], writes=[Brs])
                k.op("act", lambda e: e.activation(out=rs[:, 0:w], in_=rs[:, 0:w], func=AF.Sqrt),
                     reads=[Brs], writes=[Brs])
                k.op("dve", lambda e: e.reciprocal(out=rs[:, 0:w], in_=rs[:, 0:w]), reads=[Brs], writes=[Brs])
                for f in range(FT):
                    outap = XN[:, f, t0:t0 + w] if to_dram is None else xi[:, f, 0:w]
                    wr_ = [B_XN[f]] if to_dram is None else [bx]
                    if f % 3 != 2:
                        k.op("dve", lambda e, f=f, outap=outap: e.scalar_tensor_tensor(
                            out=outap, in0=xi[:, f, 0:w], scalar=gam[:, gi, f:f + 1],
                            in1=rs[:, 0:w], op0=ALU.mult, op1=ALU.mult),
                             reads=[bx, Brs, B_const], writes=wr_)
                    else:
                        k.op("pool", lambda e, f=f: e.tensor_scalar(out=xi[:, f, 0:w], in0=xi[:, f, 0:w],
                                                                    scalar1=gam[:, gi, f:f + 1], scalar2=None, op0=ALU.mult),
                             reads=[bx, B_const], writes=[bx])
                        k.op("pool", lambda e, f=f, outap=outap: e.tensor_tensor(out=outap, in0=xi[:, f, 0:w], in1=rs[:, 0:w],
                                                                                 op=ALU.mult),
                             reads=[bx, Brs], writes=wr_)
                if to_dram is not None:
                    k.dma("sp", to_dram.rearrange("(f p) t -> p f t", p=128)[:, :, t0:t0 + w], xi[:, :, 0:w], [bx], [B_out])
            k.barrier()

    def linear(Ws, n_out, epilogue, tiles=TT, kin=FT, tag="l"):
        nW = len(Ws)
        with ExitStack() as sc:
            nslot = 2
            wb = [[alloc(sc, "%sw" % tag, [128, kin, 512], BF16) for s in range(nslot)] for j in range(nW)]
            Bw = [[B("%sw%d_%d" % (tag, j, s)) for s in range(nslot)] for j in range(nW)]
            nch = n_out // 512
            for c in range(nch):
                s = c % nslot
                for j in range(nW):
                    srcw = Ws[j].rearrange("(f p) n -> p f n", p=128)[:, :, c * 512:(c + 1) * 512]
                    k.dma("pool", wb[j][s][:], srcw, [], [Bw[j][s]])
                for oo in range(4):
                    o = c * 4 + oo
                    for ti, (t0, w) in enumerate(tiles):
                        pts = []
                        for j in range(nW):
                            pt, bp = k.next_psum()
                            for f in range(kin):
                                k.op("pe", lambda e, f=f, j=j, pt=pt: e.matmul(
                                    pt[:, 0:w], wb[j][s][:, f, oo * 128:(oo + 1) * 128], XN[:, f, t0:t0 + w],
                                    start=(f == 0), stop=(f == kin - 1)),
                                     reads=[Bw[j][s], B_XN[f]], writes=[bp], inc=(f == kin - 1))
                            pts.append((pt, bp))
                        epilogue(o, ti, t0, w, pts)
            k.barrier()

    class V:
        def __init__(self, t, buf):
            self.t = t
            self.buf = buf

        def __getitem__(self, idx):
            return self.t[idx]

    def resid_linear(Ws, combine, tag):
        s, d_ = cur[0], 1 - cur[0]
        with ExitStack() as sc:
            rts = [V(alloc(sc, tag + "r", [128, 512], F32), B("%sr%d" % (tag, i))) for i in range(3)]
            ots = [V(alloc(sc, tag + "o", [128, 512], F32), B("%so%d" % (tag, i))) for i in range(3)]
            tms = [V(alloc(sc, tag + "t", [128, 512], F32), B("%st%d" % (tag, i))) for i in range(3)]
            cnt = [0]

            def ep(o, ti, t0, w, pts):
                i = cnt[0] % 3
                cnt[0] += 1
                rt, ot, tm = rts[i], ots[i], tms[i]
                k.dma("sp", rt[:, 0:w], XT[s][o * 128:(o + 1) * 128, t0:t0 + w], [B_XT[s]], [rt.buf])
                combine(w, pts, rt, ot, tm)
                k.dma("sp", XT[d_][o * 128:(o + 1) * 128, t0:t0 + w], ot[:, 0:w], [ot.buf], [B_XT[d_]])

            linear(Ws, D, ep, tag=tag)
        cur[0] = d_

    def attn_layer(li, j):
        norm_phase(li)
        if CFG["stop"] == "norm":
            raise Stop()
        with ExitStack() as sc:
            st = [alloc(sc, "qs", [128, 512], BF16) for i in range(4)]
            Bst = [B("qs%d" % i) for i in range(4)]
            sf = [alloc(sc, "qf", [128, 512], F32) for i in range(2)]
            Bsf = [B("qf%d" % i) for i in range(2)]
            cnt = [0, 0]

            def ep(o, ti, t0, w, pts):
                pt, bp = pts[0]
                which, of = o // FT, o % FT
                i = cnt[0] % 4
                cnt[0] += 1
                kvout = which >= 1 and ti >= 3
                if kvout:
                    i2 = cnt[1] % 2
                    cnt[1] += 1
                    k.op("dve", lambda e: e.tensor_copy(out=sf[i2][:, 0:w], in_=pt[:, 0:w]), reads=[bp], writes=[Bsf[i2]])
                    k.op("pool", lambda e: e.tensor_copy(out=st[i][:, 0:w], in_=sf[i2][:, 0:w]), reads=[Bsf[i2]], writes=[Bst[i]])
                    c0 = 0 if ti == 3 else 512
                    k.dma("sp", KVo[which - 1, of * 128:(of + 1) * 128, c0:c0 + w], sf[i2][:, 0:w], [Bsf[i2]], [B_out])
                elif cnt[0] % 2 == 0:
                    k.op("act", lambda e: e.copy(out=st[i][:, 0:w], in_=pt[:, 0:w]), reads=[bp], writes=[Bst[i]])
                else:
                    k.op("dve", lambda e: e.tensor_copy(out=st[i][:, 0:w], in_=pt[:, 0:w]), reads=[bp], writes=[Bst[i]])
                k.dma("sp", QKV[which, of * 128:(of + 1) * 128, t0:t0 + w], st[i][:, 0:w], [Bst[i]], [B_QKV])
                if kvout and ti == 3:
                    k.dma("sp", cin.ap()[(which - 1) * D + of * 128:(which - 1) * D + (of + 1) * 128, :],
                          st[i][:, 0:512], [Bst[i]], [B_cin])

            linear([wqkv[j]], 3 * D, ep, tag="q")
        k.dma("sp", kT_out[j], KVo[0], [], [B("kvo_k")])
        k.dma("sp", vT_out[j], KVo[1], [], [B("kvo_v")])
        if CFG["stop"] in ("qkv",):
            raise Stop()
        k.op("pool", lambda e: e.collective_compute("AllGather", ALU.bypass, replica_groups=[list(range(NCORES))],
                                                    ins=[cin.ap().opt()], outs=[cout.ap().opt()]),
             reads=[B_cin], writes=[B_cout])
        k.barrier()
        if CFG["stop"] == "xchg":
            raise Stop()
        attention_core(j)
        if CFG["stop"] == "attn":
            raise Stop()

        def comb(w, pts, rt, ot, tm):
            k.op("dve", lambda e: e.tensor_tensor(out=ot[:, 0:w], in0=pts[0][0][:, 0:w], in1=rt[:, 0:w], op=ALU.add),
                 reads=[pts[0][1], rt.buf], writes=[ot.buf])

        resid_linear([wo[j]], comb, tag="o")
        if CFG["stop"] == "wo":
            raise Stop()

    def attention_core(j):
        scale = 128 ** -0.5
        with ExitStack() as sc:
            A = lambda name, shape, dtype: alloc(sc, name, shape, dtype)
            qT = A("a_q", [128, NT], BF16); Bq = B("a_q")
            KP = A("a_kp", [128, 2560], BF16); Bkp = B("a_kp")
            KS = A("a_ks", [128, 576], BF16); Bks = B("a_ks")
            vT = A("a_vt", [128, 2560 + 64], BF16); Bvt = B("a_vt")
            VP = A("a_vp", [128, 21, 128], BF16); Bvp = B("a_vp")
            VC = A("a_vc", [128, 4, 128], BF16); Bvc = B("a_vc")
            G = A("a_g", [128, 8, 512], BF16); Bg = B("a_g")
            acc = A("a_acc", [128, 512], F32); Bacc = B("a_acc")
            EBr = A("a_ebr", [128, 5, 128], F32); Bebr = B("a_ebr")
            EB = A("a_eb", [128, 5, 128], F32); Beb = B("a_eb")
            EBh = A("a_ebh", [128, 5, 128], F32); Bebh = B("a_ebh")
            ES = A("a_es", [128, 5, 128], F32); Bes = B("a_es")
            Eb = A("a_e", [128, 5, 128], BF16); Be = B("a_e")
            rl = A("a_rl", [128, 128], F32); Brl = B("a_rl")
            M0 = A("a_m0", [128, 128], F32); M4 = A("a_m4", [128, 128], F32); Bm = B("a_m")
            k.op("pool", lambda e: e.memset(M0[:], 1.0), writes=[Bm])
            k.op("pool", lambda e: e.memset(M0[0:64, 64:128], 0.0), reads=[Bm], writes=[Bm])
            k.op("pool", lambda e: e.memset(M4[:], 1.0), reads=[Bm], writes=[Bm])
            k.op("pool", lambda e: e.memset(M4[64:128, 0:64], 0.0), reads=[Bm], writes=[Bm])
            coutv = cout.ap().rearrange("(r kv f) t -> kv f r t", r=NCORES, kv=2)
            for h in range(16):
                hs = slice(h * 128, (h + 1) * 128)
                k.dma("sp", qT[:], QKV[0, hs, :], [B_QKV], [Bq])
                k.dma("sp", KP[:, 512:2560], QKV[1, hs, 0:NP], [B_QKV], [Bkp])
                k.dma("sp", KS[:, 512:576], QKV[1, hs, NP:NT], [B_QKV], [Bks])
                k.dma("pool", KS[:, 0:512], ckT[j, hs, :], [], [Bks])
                k.dma("sp", vT[:, 512:2624], QKV[2, hs, :], [B_QKV], [Bvt])
                k.dma("pool", VC[:], cv[j].rearrange("(t p) f -> p t f", p=128)[:, :, hs], [], [Bvc])
                k.dma("sp", EBr[:], ebraw[j, h], [], [Bebr])
                k.op("act", lambda e: e.activation(out=EB[:], in_=EBr[:], func=AF.Exp), reads=[Bebr], writes=[Beb])
                k.op("dve", lambda e: e.tensor_tensor(out=EB[:, 0, :], in0=EB[:, 0, :], in1=M0[:], op=ALU.mult),
                     reads=[Beb, Bm], writes=[Beb])
                k.op("dve", lambda e: e.tensor_tensor(out=EB[:, 4, :], in0=EB[:, 4, :], in1=M4[:], op=ALU.mult),
                     reads=[Beb, Bm], writes=[Beb])
                k.op("dve", lambda e: e.tensor_scalar(out=EBh[:], in0=EB[:], scalar1=hv_sb[:, 0:1], scalar2=None,
                                                      op0=ALU.mult), reads=[Beb, B_const], writes=[Bebh])
                for kv in range(2):
                    k.dma("sp", G[:], coutv[kv, hs, :, :], [B_cout], [Bg])
                    dstT = KP if kv == 0 else vT
                    bdst = Bkp if kv == 0 else Bvt
                    k.op("dve", lambda e: e.tensor_scalar(out=acc[:], in0=G[:, 0, :], scalar1=sel_sb[:, 0:1],
                                                          scalar2=None, op0=ALU.mult),
                         reads=[Bg, B_const], writes=[Bacc])
                    for r in range(1, 8):
                        last = (r == 7)
                        k.op("dve", lambda e, r=r, last=last, dstT=dstT: e.scalar_tensor_tensor(
                            out=(dstT[:, 0:512] if last else acc[:]), in0=G[:, r, :], scalar=sel_sb[:, r:r + 1],
                            in1=acc[:], op0=ALU.mult, op1=ALU.add),
                             reads=[Bg, Bacc, B_const], writes=[bdst if last else Bacc])
                for t in range(21):
                    wv = 128 if t < 20 else 64
                    pt, bp = k.next_psum()
                    pv = pt[:].bitcast(BF16)
                    k.op("pe", lambda e, t=t, wv=wv, pv=pv: e.transpose(pv[0:wv, 0:128], vT[:, t * 128:t * 128 + wv], ident[:]),
                         reads=[Bvt, B_const], writes=[bp])
                    if t % 2:
                        k.op("act", lambda e, t=t, wv=wv, pv=pv: e.copy(out=VP[0:wv, t, :], in_=pv[0:wv, 0:128]),
                             reads=[bp], writes=[Bvp])
                    else:
                        k.op("dve", lambda e, t=t, wv=wv, pv=pv: e.tensor_copy(out=VP[0:wv, t, :], in_=pv[0:wv, 0:128]),
                             reads=[bp], writes=[Bvp])
                for qi in range(17):
                    samp = (qi == 16)
                    qw = 64 if samp else 128
                    q0 = NP if samp else qi * 128
                    k5 = 64 if samp else 128
                    pS, bS = k.next_psum()
                    pS2, bS2 = k.next_psum()
                    for o in range(5):
                        if samp:
                            kw_ = 128 if o < 4 else 64
                            kl = KS[:, o * 128:o * 128 + kw_]
                            rb = Bks
                        else:
                            kw_ = 128
                            kl = KP[:, (qi + o) * 128:(qi + o + 1) * 128]
                            rb = Bkp
                        po, bo = (pS, bS) if o < 4 else (pS2, bS2)
                        oc = (o % 4) * 128
                        k.op("pe", lambda e, kl=kl, po=po, oc=oc, kw_=kw_: e.matmul(
                            po[0:kw_, oc:oc + qw], kl, qT[:, q0:q0 + qw], start=True, stop=True),
                             reads=[rb, Bq], writes=[bo], inc=(o >= 3))
                    k.op("act", lambda e: e.activation(out=ES[:, 0:4, 0:qw],
                                                       in_=pS[:, 0:512].rearrange("p (o q) -> p o q", o=4)[:, :, 0:qw],
                                                       func=AF.Exp, scale=scale), reads=[bS], writes=[Bes])
                    k.op("act", lambda e: e.activation(out=ES[0:k5, 4, 0:qw], in_=pS2[0:k5, 0:qw], func=AF.Exp, scale=scale),
                         reads=[bS2], writes=[Bes])
                    for o in range(5):
                        kk = k5 if o == 4 else 128
                        if samp:
                            ebt = EB[:, 1, :] if o == 0 else EB[:, o, :]
                            beb = Beb
                        else:
                            halo = (qi + o) < 4
                            ebt = (EBh if halo else EB)[:, o, :]
                            beb = Bebh if halo else Beb
                        k.op("dve" if o % 2 == 0 else "pool", lambda e, o=o, kk=kk, ebt=ebt: e.tensor_tensor(
                            out=Eb[0:kk, o, 0:qw], in0=ES[0:kk, o, 0:qw], in1=ebt[0:kk, 0:qw], op=ALU.mult),
                             reads=[Bes, beb], writes=[Be])
                    pO, bO = k.next_psum()
                    pL, bL = k.next_psum()
                    for o in range(5):
                        kk = k5 if o == 4 else 128
                        if samp:
                            vl = VC[:, o, :] if o < 4 else VP[0:64, 20, :]
                            rb = Bvc if o < 4 else Bvp
                        else:
                            vl = VP[:, qi + o, :]
                            rb = Bvp
                        k.op("pe", lambda e, o=o, kk=kk, vl=vl: e.matmul(pO[:, 0:qw], vl, Eb[0:kk, o, 0:qw],
                                                                         start=(o == 0), stop=(o == 4)),
                             reads=[rb, Be], writes=[bO], inc=(o == 4))
                    for o in range(5):
                        kk = k5 if o == 4 else 128
                        k.op("pe", lambda e, o=o, kk=kk: e.matmul(pL[:, 0:qw], ones[0:kk, :], Eb[0:kk, o, 0:qw],
                                                                  start=(o == 0), stop=(o == 4)),
                             reads=[B_const, Be], writes=[bL], inc=(o == 4))
                    k.op("dve", lambda e: e.reciprocal(out=rl[:, 0:qw], in_=pL[:, 0:qw]), reads=[bL], writes=[Brl])
                    k.op("dve", lambda e: e.tensor_tensor(out=XN[:, h, q0:q0 + qw], in0=pO[:, 0:qw], in1=rl[:, 0:qw], op=ALU.mult),
                         reads=[bO, Brl], writes=[B_XN[h]])
            k.barrier()

    def mlp_layer(li):
        norm_phase(4 + li)
        halves = [[(0, 512), (512, 512)], [(1024, 512), (1536, 512), (2048, 64)]]
        s, d_ = cur[0], 1 - cur[0]
        with ExitStack() as sc:
            A = lambda name, shape, dtype: alloc(sc, name, shape, dtype)
            acc = A("m_acc", [128, FT, 1088], F32)
            Bacc = [B("m_acc%d" % f) for f in range(FT)]
            wu = [A("m_wu", [128, FT, 256], BF16) for i in range(2)]
            Bwu = [B("m_wu%d" % i) for i in range(2)]
            wd = [A("m_wd", [128, 2, D], BF16) for i in range(2)]
            Bwd = [B("m_wd%d" % i) for i in range(2)]
            hb = [A("m_h", [128, 2, 1088], BF16) for i in range(2)]
            Bh = [B("m_h%d" % i) for i in range(2)]
            rt = [A("m_r", [128, 512], F32) for i in range(2)]
            Brt = [B("m_r%d" % i) for i in range(2)]
            rc = [0]
            for hh, tiles in enumerate(halves):
                base = tiles[0][0]
                wtot = sum(w for _, w in tiles)
                k.dma("sp", acc[:, :, 0:wtot], XT[s].rearrange("(f p) t -> p f t", p=128)[:, :, base:base + wtot],
                      [B_XT[s]], Bacc)
                for c in range(DFF // 256):
                    sl = c % 2
                    k.dma("pool", wu[sl][:], wup[li].rearrange("(f p) n -> p f n", p=128)[:, :, c * 256:(c + 1) * 256],
                          [], [Bwu[sl]])
                    k.dma("pool", wd[sl][:], wdn[li, c * 256:(c + 1) * 256, :].rearrange("(f p) n -> p f n", p=128),
                          [], [Bwd[sl]])
                    for hk in range(2):
                        for (t0, w) in tiles:
                            pt, bp = k.next_psum()
                            for f in range(FT):
                                k.op("pe", lambda e, f=f, pt=pt: e.matmul(pt[:, 0:w], wu[sl][:, f, hk * 128:(hk + 1) * 128],
                                                                          XN[:, f, t0:t0 + w], start=(f == 0), stop=(f == FT - 1)),
                                     reads=[Bwu[sl], B_XN[f]], writes=[bp], inc=(f == FT - 1))
                            ri = rc[0] % 2
                            rc[0] += 1
                            k.op("act", lambda e, pt=pt, ri=ri: e.activation(out=rt[ri][:, 0:w], in_=pt[:, 0:w], func=AF.Relu),
                                 reads=[bp], writes=[Brt[ri]])
                            k.op("pool", lambda e, ri=ri: e.tensor_tensor(out=hb[sl][:, hk, t0 - base:t0 - base + w],
                                                                          in0=rt[ri][:, 0:w], in1=rt[ri][:, 0:w], op=ALU.mult),
                                 reads=[Brt[ri]], writes=[Bh[sl]])
                    for o in range(FT):
                        for (t0, w) in tiles:
                            pt, bp = k.next_psum()
                            for hk in range(2):
                                k.op("pe", lambda e, hk=hk, pt=pt: e.matmul(pt[:, 0:w], wd[sl][:, hk, o * 128:(o + 1) * 128],
                                                                            hb[sl][:, hk, t0 - base:t0 - base + w],
                                                                            start=(hk == 0), stop=(hk == 1)),
                                     reads=[Bwd[sl], Bh[sl]], writes=[bp], inc=(hk == 1))
                            k.op("dve", lambda e, pt=pt: e.tensor_tensor(out=acc[:, o, t0 - base:t0 - base + w],
                                                                         in0=acc[:, o, t0 - base:t0 - base + w],
                                                                         in1=pt[:, 0:w], op=ALU.add),
                                 reads=[bp, Bacc[o]], writes=[Bacc[o]])
                k.dma("sp", XT[d_].rearrange("(f p) t -> p f t", p=128)[:, :, base:base + wtot], acc[:, :, 0:wtot],
                      Bacc, [B_XT[d_]])
            k.barrier()
        cur[0] = d_

    def ssm_layer(li, j):
        norm_phase(li)
        ssm_tables(j)
        ssm_core(j)

        def comb(w, pts, rt, ot, tm):
            k.op("act", lambda e: e.activation(out=tm[:, 0:w], in_=pts[1][0][:, 0:w], func=AF.Sigmoid),
                 reads=[pts[1][1]], writes=[tm.buf])
            k.op("dve", lambda e: e.tensor_tensor(out=tm[:, 0:w], in0=pts[0][0][:, 0:w], in1=tm[:, 0:w], op=ALU.mult),
                 reads=[pts[0][1], tm.buf], writes=[tm.buf])
            k.op("pool", lambda e: e.tensor_tensor(out=ot[:, 0:w], in0=tm[:, 0:w], in1=rt[:, 0:w], op=ALU.add),
                 reads=[tm.buf, rt.buf], writes=[ot.buf])

        resid_linear([wga[j], wgb[j]], comb, tag="g")

    TG = B("TG")

    def tg(eng, fn):
        return k.op(eng, fn, reads=[TG], writes=[TG])

    def ssm_tables(j):
        with ExitStack() as sc:
            A = lambda name, shape, dtype: alloc(sc, name, shape, dtype)
            are = A("t_are", [128, 64], F32); aim = A("t_aim", [128, 64], F32); ldt = A("t_ldt", [128, 1], F32)
            dts = A("t_dts", [128, 1], F32); hpi = A("t_hpi", [128, 1], F32)
            LP = A("t_lp", [128, 2, 9, 64], F32)
            LI = A("t_li", [128, 2, 9, 64], F32)
            PW = A("t_pw", [128, 6, 64], F32)
            t1 = A("t_t1", [128, 1024], F32); t2 = A("t_t2", [128, 1024], F32)
            u1 = A("t_u1", [128, 64], F32); u2 = A("t_u2", [128, 64], F32); u3 = A("t_u3", [128, 64], F32)
            u4 = A("t_u4", [128, 64], F32)
            zr = A("t_zr", [128, 64], F32); zi = A("t_zi", [128, 64], F32)
            bre = A("t_bre", [128, 64, 16], F32); bim = A("t_bim", [128, 64, 16], F32)
            cre = A("t_cre", [128, 16, 64], F32); cim = A("t_cim", [128, 16, 64], F32)
            BR = A("t_BR", [128, 64, 16], F32); BI = A("t_BI", [128, 64, 16], F32)
            tab = A("t_tab", [128, 16384], BF16)
            for dst, src in ((are, s_are[j]), (aim, s_aim[j]), (ldt, s_ldt[j]),
                             (bre, s_bre[j].rearrange("g (p s) -> g p s", s=16)), (bim, s_bim[j].rearrange("g (p s) -> g p s", s=16)),
                             (cre, s_cre[j].rearrange("g (s p) -> g s p", p=64)), (cim, s_cim[j].rearrange("g (s p) -> g s p", p=64))):
                k.dma("sp", dst[:], src, [], [TG])
            tg("pool", lambda e: e.memset(hpi[:], float(np.pi / 2)))
            tg("act", lambda e: e.activation(out=dts[:], in_=ldt[:], func=AF.Exp))
            tg("dve", lambda e: e.tensor_scalar(out=dts[:], in0=dts[:], scalar1=1.0 / 32, scalar2=None, op0=ALU.mult))
            tg("act", lambda e: e.activation(out=u1[:], in_=are[:], func=AF.Exp, scale=dts[:, 0:1]))
            tg("act", lambda e: e.activation(out=u2[:], in_=aim[:], func=AF.Sin, scale=dts[:, 0:1]))
            tg("act", lambda e: e.activation(out=u3[:], in_=aim[:], func=AF.Sin, scale=dts[:, 0:1], bias=hpi[:, 0:1]))
            Lr = lambda kk: LP[:, 0, kk, :]
            Li = lambda kk: LP[:, 1, kk, :]
            Ir = lambda kk: LI[:, 0, kk, :]
            Ii = lambda kk: LI[:, 1, kk, :]
            tg("dve", lambda e: e.tensor_tensor(out=Lr(1), in0=u1[:], in1=u3[:], op=ALU.mult))
            tg("dve", lambda e: e.tensor_tensor(out=Li(1), in0=u1[:], in1=u2[:], op=ALU.mult))

            def cmul(outr, outi, ar, ai, br, bi, ta, tb, neg_im=False):
                tg("dve", lambda e: e.tensor_tensor(out=ta, in0=ar, in1=br, op=ALU.mult))
                tg("dve", lambda e: e.tensor_tensor(out=tb, in0=ai, in1=bi, op=ALU.mult))
                tg("dve", lambda e: e.tensor_tensor(out=outr, in0=ta, in1=tb, op=ALU.subtract))
                tg("dve", lambda e: e.tensor_tensor(out=ta, in0=ar, in1=bi, op=ALU.mult))
                tg("dve", lambda e: e.tensor_tensor(out=tb, in0=ai, in1=br, op=ALU.mult))
                if neg_im:
                    tg("dve", lambda e: e.scalar_tensor_tensor(out=outi, in0=ta, scalar=-1.0, in1=tb, op0=ALU.mult, op1=ALU.subtract))
                else:
                    tg("dve", lambda e: e.tensor_tensor(out=outi, in0=ta, in1=tb, op=ALU.add))

            def csq_inplace(r, i):
                cmul(u3[:], u4[:], r, i, r, i, u1[:], u2[:])
                tg("dve", lambda e: e.tensor_copy(out=r, in_=u3[:]))
                tg("dve", lambda e: e.tensor_copy(out=i, in_=u4[:]))

            for _ in range(5):
                csq_inplace(Lr(1), Li(1))
            tg("pool", lambda e: e.memset(LP[:, 0, 0, :], 1.0))
            tg("pool", lambda e: e.memset(LP[:, 1, 0, :], 0.0))
            tg("pool", lambda e: e.memset(LI[:, 0, 0, :], 1.0))
            tg("pool", lambda e: e.memset(LI[:, 1, 0, :], 0.0))
            for kk, (a, b) in ((2, (1, 1)), (3, (2, 1)), (4, (2, 2)), (5, (4, 1)), (6, (4, 2)), (7, (4, 3)), (8, (4, 4))):
                cmul(Lr(kk), Li(kk), Lr(a), Li(a), Lr(b), Li(b), u1[:], u2[:])
            tg("dve", lambda e: e.tensor_tensor(out=u1[:], in0=Lr(1), in1=Lr(1), op=ALU.mult))
            tg("dve", lambda e: e.tensor_tensor(out=u2[:], in0=Li(1), in1=Li(1), op=ALU.mult))
            tg("dve", lambda e: e.tensor_tensor(out=u1[:], in0=u1[:], in1=u2[:], op=ALU.add))
            tg("dve", lambda e: e.reciprocal(out=u1[:], in_=u1[:]))
            tg("dve", lambda e: e.tensor_tensor(out=Ir(1), in0=Lr(1), in1=u1[:], op=ALU.mult))
            tg("dve", lambda e: e.scalar_tensor_tensor(out=Ii(1), in0=Li(1), scalar=-1.0, in1=u1[:], op0=ALU.mult, op1=ALU.mult))
            for kk, (a, b) in ((2, (1, 1)), (3, (2, 1)), (4, (2, 2)), (5, (4, 1)), (6, (4, 2)), (7, (4, 3)), (8, (4, 4))):
                cmul(Ir(kk), Ii(kk), Ir(a), Ii(a), Ir(b), Ii(b), u1[:], u2[:])
            tg("dve", lambda e: e.tensor_copy(out=PW[:, 0, :], in_=Lr(8)))
            tg("dve", lambda e: e.tensor_copy(out=PW[:, 1, :], in_=Li(8)))
            tg("dve", lambda e: e.tensor_copy(out=PW[:, 2, :], in_=Lr(8)))
            tg("dve", lambda e: e.tensor_copy(out=PW[:, 3, :], in_=Li(8)))
            for _ in range(8):
                csq_inplace(PW[:, 2, :], PW[:, 3, :])
            cmul(PW[:, 4, :], PW[:, 5, :], PW[:, 2, :], PW[:, 3, :], PW[:, 2, :], PW[:, 3, :], u1[:], u2[:])
            k.dma("sp", lamD.rearrange("k g p -> g k p"), PW[:], [TG], [B_lamD])
            tg("dve", lambda e: e.tensor_scalar(out=u1[:], in0=Lr(1), scalar1=-1.0, scalar2=None, op0=ALU.add))
            tg("dve", lambda e: e.tensor_tensor(out=u2[:], in0=are[:], in1=are[:], op=ALU.mult))
            tg("dve", lambda e: e.tensor_tensor(out=u3[:], in0=aim[:], in1=aim[:], op=ALU.mult))
            tg("dve", lambda e: e.tensor_tensor(out=u2[:], in0=u2[:], in1=u3[:], op=ALU.add))
            tg("dve", lambda e: e.reciprocal(out=u2[:], in_=u2[:]))
            tg("dve", lambda e: e.tensor_tensor(out=u3[:], in0=u1[:], in1=are[:], op=ALU.mult))
            tg("dve", lambda e: e.tensor_tensor(out=u4[:], in0=Li(1), in1=aim[:], op=ALU.mult))
            tg("dve", lambda e: e.tensor_tensor(out=u3[:], in0=u3[:], in1=u4[:], op=ALU.add))
            tg("dve", lambda e: e.tensor_tensor(out=zr[:], in0=u3[:], in1=u2[:], op=ALU.mult))
            tg("dve", lambda e: e.tensor_tensor(out=u3[:], in0=Li(1), in1=are[:], op=ALU.mult))
            tg("dve", lambda e: e.tensor_tensor(out=u4[:], in0=u1[:], in1=aim[:], op=ALU.mult))
            tg("dve", lambda e: e.tensor_tensor(out=u3[:], in0=u3[:], in1=u4[:], op=ALU.subtract))
            tg("dve", lambda e: e.tensor_tensor(out=zi[:], in0=u3[:], in1=u2[:], op=ALU.mult))
            bc_ps = lambda ap: ap.unsqueeze(2).to_broadcast([128, 64, 16])
            bc_sp = lambda ap: ap.unsqueeze(1).to_broadcast([128, 16, 64])
            T1 = t1[:].rearrange("g (p s) -> g p s", s=16)
            T2 = t2[:].rearrange("g (p s) -> g p s", s=16)
            cmul(BR[:], BI[:], bc_ps(zr[:]), bc_ps(zi[:]), bre[:], bim[:], T1, T2)
            T1s = t1[:].rearrange("g (s p) -> g s p", p=64)
            T2s = t2[:].rearrange("g (s p) -> g s p", p=64)
            BRs = BR[:].rearrange("g p s -> g s p")
            BIs = BI[:].rearrange("g p s -> g s p")
            tv = tab[:].rearrange("g (j s c) -> g j s c", j=8, s=16)
            for jj in range(8):
                cmul(tv[:, jj, :, 0:64], tv[:, jj, :, 64:128], bc_sp(Lr(7 - jj)), bc_sp(Li(7 - jj)), BRs, BIs, T1s, T2s)
            k.dma("sp", tabD[0], tab[:], [TG], [B_tabD])
            tv1 = tab[:].rearrange("g (c j s) -> g c j s", c=128, j=8)
            for jj in range(8):
                cmul(tv1[:, 0:64, jj, :], tv1[:, 64:128, jj, :], bc_ps(Ir(1 + jj)), bc_ps(Ii(1 + jj)), BR[:], BI[:], T1, T2)
            k.dma("sp", tabD[1], tab[:], [TG], [B_tabD])
            CRp = cre[:].rearrange("g s p -> g p s")
            CIp = cim[:].rearrange("g s p -> g p s")
            for tt_ in range(8):
                cmul(tv1[:, 0:64, tt_, :], tv1[:, 64:128, tt_, :], CRp, CIp, bc_ps(Lr(tt_ + 1)), bc_ps(Li(tt_ + 1)), T1, T2,
                     neg_im=True)
            k.dma("sp", tabD[2], tab[:], [TG], [B_tabD])
            k.barrier()
        with ExitStack() as sc:
            A = lambda name, shape, dtype: alloc(sc, name, shape, dtype)
            tri = A("t_tri", [128, 128], F32); Btri = B("t_tri")
            k.op("pool", lambda e: e.memset(tri[:], 1.0), writes=[Btri])
            k.op("pool", lambda e: e.affine_select(out=tri[:].rearrange("p (t s) -> p t s", t=8),
                                                   in_=tri[:].rearrange("p (t s) -> p t s", t=8),
                                                   pattern=[[16, 8], [0, 16]], compare_op=ALU.is_ge, fill=0.0, base=15,
                                                   channel_multiplier=-1), reads=[Btri], writes=[Btri])
            opAs = [A("t_opa", [128, 16, 128], BF16) for _ in range(2)]
            opCs = [A("t_opc", [128, 16, 128], BF16) for _ in range(2)]
            t0os = [A("t_t0o", [128, 16, 128], BF16) for _ in range(2)]
            for gc in range(8):
                g0 = gc * 16
                opA = opAs[gc % 2]; BA_ = B("t_opa%d" % (gc % 2))
                opC = opCs[gc % 2]; BC_ = B("t_opc%d" % (gc % 2))
                t0o = t0os[gc % 2]; BT_ = B("t_t0o%d" % (gc % 2))
                k.dma("sp", opA[:], tabD[1, g0:g0 + 16].rearrange("g (c r) -> c g r", r=128), [B_tabD], [BA_])
                k.dma("sp", opC[:], tabD[2, g0:g0 + 16].rearrange("g (c r) -> c g r", r=128), [B_tabD], [BC_])
                for q in range(4):
                    pt, bp = k.next_psum()
                    for gi in range(4):
                        g = q * 4 + gi
                        k.op("pe", lambda e, g=g, gi=gi, pt=pt: e.matmul(pt[:, gi * 128:(gi + 1) * 128], opA[:, g, :], opC[:, g, :],
                                                                         start=True, stop=True), reads=[BA_, BC_], writes=[bp], inc=(gi == 3))
                    k.op("dve", lambda e, q=q, pt=pt: e.tensor_tensor(
                        out=t0o[:, q * 4:(q + 1) * 4, :], in0=pt[:].rearrange("p (g c) -> p g c", g=4),
                        in1=tri[:].unsqueeze(1).to_broadcast([128, 4, 128]), op=ALU.mult), reads=[bp, Btri], writes=[BT_])
                k.dma("sp", tabD[3, g0:g0 + 16].rearrange("g (r c) -> r g c", c=128), t0o[:], [BT_], [B_tabD])
            k.barrier()

    def ssm_core(j):
        nchunks = 8
        with ExitStack() as sc:
            A = lambda name, shape, dtype: alloc(sc, name, shape, dtype)
            Sel = A("s_sel", [128, 8, 8, 128], BF16); Bsel = B("s_sel")
            PW = A("s_pw", [128, 6, 64], F32); Bpw = B("s_pw")
            SS = A("s_ss", [128, NB, 2, 64], F32); Bss = B("s_ss")
            XH = A("s_xh", [128, NB + 1, 2, 64], F32); Bxh = B("s_xh")
            X0 = A("s_x0", [128, 64, 2, NB], BF16); Bx0 = B("s_x0")
            UM = A("s_um", [128, 128, NB], BF16); Bum = B("s_um")
            YM = [A("s_ym", [128, 16, NB], BF16) for _ in range(2)]; Bym = [B("s_ym0"), B("s_ym1")]
            opAB = [A("s_oab", [128, 2, 8, 128], BF16) for _ in range(2)]; Bab = [B("s_oab0"), B("s_oab1")]
            opT0 = [A("s_ot0", [128, 2, 8, 128], BF16) for _ in range(2)]; Bt0 = [B("s_ot00"), B("s_ot01")]
            opCA = [A("s_oca", [128, 8, 2, 128], BF16) for _ in range(2)]; Bca = [B("s_oca0"), B("s_oca1")]
            w1 = A("s_w1", [128, 64], F32); w2 = A("s_w2", [128, 64], F32); w3 = A("s_w3", [128, 64], F32); w4 = A("s_w4", [128, 64], F32)
            Bw = B("s_w"); Bwp = B("s_wp"); Bxhi = B("s_xhi")
            Fst = A("s_f", [128, 2, 64], F32); Bf = B("s_f")
            Gs = A("s_gs", [128, 8, 2, 64], F32); Bgs = B("s_gs")
            car = A("s_car", [128, 2, 64], F32); Bcar = B("s_car")
            wr = A("s_wr", [128, 2, 64], F32); Bwr = B("s_wr")
            stt = A("s_st", [128, 2, 64], F32); Bstt = B("s_st")
            ept = [A("s_ep", [128, 8 * NB], F32) for _ in range(2)]; Bep = [B("s_ep0"), B("s_ep1")]
            k.op("pool", lambda e: e.memset(Sel[:], 0.0), writes=[Bsel])
            for a in range(8):
                sv = Sel[:, a, :, :]
                k.op("pool", lambda e, a=a, sv=sv: e.affine_select(out=sv, in_=sv, pattern=[[-16, 8], [1, 128]],
                                                                   compare_op=ALU.not_equal, fill=1.0, base=16 * a,
                                                                   channel_multiplier=-1), reads=[Bsel], writes=[Bsel])
                k.op("pool", lambda e, a=a, sv=sv: e.affine_select(out=sv, in_=sv, pattern=[[0, 8], [0, 128]],
                                                                   compare_op=ALU.is_ge, fill=0.0, base=-16 * a,
                                                                   channel_multiplier=1), reads=[Bsel], writes=[Bsel])
                k.op("pool", lambda e, a=a, sv=sv: e.affine_select(out=sv, in_=sv, pattern=[[0, 8], [0, 128]],
                                                                   compare_op=ALU.is_ge, fill=0.0, base=16 * a + 15,
                                                                   channel_multiplier=-1), reads=[Bsel], writes=[Bsel])
            with nc.allow_non_contiguous_dma(reason="tiny table relayout"):
                for par in range(2):
                    for kk in range(6):
                        for q4 in range(4):
                            k.dma("sp", PW[par * 64:(par + 1) * 64, kk, q4 * 16:(q4 + 1) * 16],
                                  lamD[kk, par * 64 + q4 * 16:par * 64 + (q4 + 1) * 16, :].rearrange("g p -> p g"),
                                  [B_lamD], [Bpw])
            k.dma("sp", stt[:], st_in[j], [], [Bstt])
            Lr8, Li8 = PW[:, 0, :], PW[:, 1, :]

            def chunk_tokens(ci):
                if ci < nchunks:
                    return ci * NB * 8, NB
                return NP, 8

            def load_ops(gc, what):
                sl = gc % 2
                q0 = gc * 8
                for par in range(2):
                    g0 = par * 64 + q0
                    if "ab" in what:
                        k.dma("sp", opAB[sl][:, par, :, :], tabD[0, g0:g0 + 8].rearrange("g (r c) -> r g c", c=128),
                              [B_tabD], [Bab[sl]])
                    if "t0" in what:
                        k.dma("sp", opT0[sl][:, par, :, :], tabD[3, g0:g0 + 8].rearrange("g (r c) -> r g c", c=128),
                              [B_tabD], [Bt0[sl]])
                    if "ca" in what:
                        k.dma("sp", opCA[sl][par * 64:(par + 1) * 64, :, :, :],
                              tabD[2, g0:g0 + 8].rearrange("g (ri p c) -> p g ri c", ri=2, p=64), [B_tabD], [Bca[sl]])
                return sl, q0

            def chunk_states(ci):
                t0, nb = chunk_tokens(ci)
                for gc in range(8):
                    sl, q0 = load_ops(gc, ("ab",))
                    pt, bp = k.next_psum()
                    for par in range(2):
                        for pi in range(8):
                            g = par * 64 + q0 + pi
                            f, gin = g // 8, g % 8
                            xv = XN[:, f, t0:t0 + nb * 8].rearrange("p (b j) -> p j b", j=8)
                            col = (par * 8 + pi) * NB
                            for jj in range(8):
                                k.op("pe", lambda e, gin=gin, jj=jj, xv=xv, col=col, pt=pt: e.matmul(
                                    pt[:, col:col + nb], Sel[:, gin, jj, :], xv[:, jj, :], start=(jj == 0), stop=(jj == 7)),
                                     reads=[Bsel, B_XN[f]], writes=[bp], inc=(par == 1 and pi == 7 and jj == 7))
                    for par in range(2):
                        k.op("act", lambda e, par=par, pt=pt: e.copy(
                            out=UM[:, par * 64 + q0:par * 64 + q0 + 8, 0:nb],
                            in_=pt[:, par * 8 * NB:(par + 1) * 8 * NB].rearrange("p (g b) -> p g b", b=NB)[:, :, 0:nb]),
                             reads=[bp], writes=[Bum])
                    pt2, bp2 = k.next_psum()
                    for par in range(2):
                        for pi in range(8):
                            g = par * 64 + q0 + pi
                            for ri in range(2):
                                col = (pi * 2 + ri) * NB
                                k.op("pe", lambda e, par=par, pi=pi, ri=ri, g=g, col=col, pt2=pt2: e.matmul(
                                    pt2[par * 64:(par + 1) * 64, col:col + nb], opAB[sl][:, par, pi, ri * 64:(ri + 1) * 64],
                                    UM[:, g, 0:nb], start=True, stop=True), reads=[Bab[sl], Bum], writes=[bp2], inc=(par == 1 and pi == 7 and ri == 1))
                    k.op("dve", lambda e, pt2=pt2, q0=q0: e.tensor_copy(
                        out=SS[:, 0:nb, :, q0:q0 + 8].rearrange("p b r q -> p q r b"),
                        in_=pt2[:].rearrange("p (q r b) -> p q r b", q=8, r=2)[:, :, :, 0:nb]), reads=[bp2], writes=[Bss])

            def recur(nb, init_r, init_i, rd):
                k.op("dve", lambda e: e.tensor_copy(out=XH[:, 0, 0, :], in_=init_r), reads=rd + [Bxh, Bxhi], writes=[Bxh])
                k.op("pool", lambda e: e.tensor_copy(out=XH[:, 0, 1, :], in_=init_i), reads=rd + [Bxh, Bxhi], writes=[Bxhi])
                for b in range(nb):
                    xr, xi = XH[:, b, 0, :], XH[:, b, 1, :]
                    k.op("dve", lambda e, xr=xr: e.tensor_tensor(out=w1[:], in0=Lr8, in1=xr, op=ALU.mult), reads=[Bpw, Bxh, Bw], writes=[Bw])
                    k.op("dve", lambda e, xi=xi: e.tensor_tensor(out=w2[:], in0=Li8, in1=xi, op=ALU.mult), reads=[Bpw, Bxhi, Bw], writes=[Bw])
                    k.op("dve", lambda e: e.tensor_tensor(out=w1[:], in0=w1[:], in1=w2[:], op=ALU.subtract), reads=[Bw], writes=[Bw])
                    k.op("dve", lambda e, b=b: e.tensor_tensor(out=XH[:, b + 1, 0, :], in0=w1[:], in1=SS[:, b, 0, :], op=ALU.add),
                         reads=[Bw, Bss, Bxh], writes=[Bxh])
                    k.op("pool", lambda e, xi=xi: e.tensor_tensor(out=w3[:], in0=Lr8, in1=xi, op=ALU.mult), reads=[Bpw, Bxhi, Bwp], writes=[Bwp])
                    k.op("pool", lambda e, xr=xr: e.tensor_tensor(out=w4[:], in0=Li8, in1=xr, op=ALU.mult), reads=[Bpw, Bxh, Bwp], writes=[Bwp])
                    k.op("pool", lambda e: e.tensor_tensor(out=w3[:], in0=w3[:], in1=w4[:], op=ALU.add), reads=[Bwp], writes=[Bwp])
                    k.op("pool", lambda e, b=b: e.tensor_tensor(out=XH[:, b + 1, 1, :], in0=w3[:], in1=SS[:, b, 1, :], op=ALU.add),
                         reads=[Bwp, Bss, Bxhi], writes=[Bxhi])

            def chunk_outputs(ci):
                t0, nb = chunk_tokens(ci)
                k.op("act", lambda e: e.copy(out=X0[:, :, :, 0:nb], in_=XH[:, 0:nb, :, :].rearrange("p b r q -> p q r b")),
                     reads=[Bxh, Bxhi], writes=[Bx0])
                for gc in range(8):
                    sl, q0 = load_ops(gc, ("t0", "ca"))
                    pt, bp = k.next_psum()
                    for par in range(2):
                        for pi in range(8):
                            g = par * 64 + q0 + pi
                            col = (par * 8 + pi) * NB
                            pr = slice(par * 64, (par + 1) * 64)
                            k.op("pe", lambda e, par=par, pi=pi, g=g, col=col, pt=pt: e.matmul(
                                pt[:, col:col + nb], opT0[sl][:, par, pi, :], UM[:, g, 0:nb], start=True, stop=False),
                                 reads=[Bt0[sl], Bum], writes=[bp], inc=False)
                            for ri in range(2):
                                k.op("pe", lambda e, pr=pr, pi=pi, ri=ri, col=col, pt=pt, q0=q0: e.matmul(
                                    pt[:, col:col + nb], opCA[sl][pr, pi, ri, :], X0[pr, q0 + pi, ri, 0:nb],
                                    start=False, stop=(ri == 1)), reads=[Bca[sl], Bx0], writes=[bp], inc=(par == 1 and pi == 7 and ri == 1))
                    ym, bym = YM[gc % 2], Bym[gc % 2]
                    k.op("act", lambda e, pt=pt, ym=ym: e.copy(out=ym[:, :, 0:nb],
                                                               in_=pt[:].rearrange("p (g b) -> p g b", b=NB)[:, :, 0:nb]),
                         reads=[bp], writes=[bym])
                    for par in range(2):
                        f = par * 8 + gc
                        pf, bpf = k.next_psum()
                        for t in range(8):
                            for gin in range(8):
                                k.op("pe", lambda e, t=t, gin=gin, par=par, pf=pf, ym=ym: e.matmul(
                                    pf[:, t * NB:t * NB + nb], Sel[:, t, gin, :], ym[:, par * 8 + gin, 0:nb],
                                    start=(gin == 0), stop=(gin == 7)), reads=[Bsel, bym], writes=[bpf], inc=(t == 7 and gin == 7))
                        ei = (gc * 2 + par) % 2
                        ep_, bep = ept[ei], Bep[ei]
                        pfv = pf[:, 0:8 * NB].rearrange("p (t b) -> p t b", t=8)[:, :, 0:nb]
                        epv = ep_[:].rearrange("p (t b) -> p t b", t=8)[:, :, 0:nb]
                        xnv = XN[:, f, t0:t0 + nb * 8].rearrange("p (b t) -> p t b", t=8)
                        k.op("dve", lambda e, f=f, pfv=pfv, epv=epv, xnv=xnv: e.scalar_tensor_tensor(
                            out=epv, in0=xnv, scalar=dsk[:, j, f:f + 1], in1=pfv, op0=ALU.mult, op1=ALU.add),
                             reads=[B_XN[f], bpf, B_const], writes=[bep])
                        k.op("act", lambda e, epv=epv, xnv=xnv: e.activation(out=xnv, in_=epv, func=AF.Gelu_apprx_tanh),
                             reads=[bep], writes=[B_XN[f]])

            k.op("pool", lambda e: e.memset(Fst[:], 0.0), writes=[Bf])
            for ci in range(nchunks):
                chunk_states(ci)
                recur(NB, Fst[:, 0, :], Fst[:, 1, :], [Bf])
                k.op("dve", lambda e: e.tensor_copy(out=Fst[:], in_=XH[:, NB, :, :]), reads=[Bxh, Bxhi], writes=[Bf])
            k.dma("sp", cinS.ap()[0:128, :], Fst[:].rearrange("p r q -> p (r q)"), [Bf], [B_cinS])
            k.barrier()
            k.op("pool", lambda e: e.collective_compute("AllGather", ALU.bypass, replica_groups=[list(range(NCORES))],
                                                        ins=[cinS.ap().opt()], outs=[coutS.ap().opt()]),
                 reads=[B_cinS], writes=[B_coutS])
            k.barrier()
            k.dma("sp", Gs[:], coutS.ap().rearrange("(r p) (ri q) -> p r ri q", p=2048, ri=2)[0:128], [B_coutS], [Bgs])
            k.op("pool", lambda e: e.memset(car[:], 0.0), writes=[Bcar])
            T = B("s_tmpx")

            def xop(fn, rd=()):
                return k.op("dve", fn, reads=[T, Bpw, Bgs, Bwr, Bcar, B_const] + list(rd), writes=[T, Bwr, Bcar, Bw])

            for r in range(8):
                c0 = selc[:, 3 * r:3 * r + 1]; c1 = selc[:, 3 * r + 1:3 * r + 2]; c2 = selc[:, 3 * r + 2:3 * r + 3]
                xop(lambda e, c1=c1: e.tensor_scalar(out=wr[:, 0, :], in0=PW[:, 2, :], scalar1=c1, scalar2=None, op0=ALU.mult))
                xop(lambda e, c1=c1: e.tensor_scalar(out=wr[:, 1, :], in0=PW[:, 3, :], scalar1=c1, scalar2=None, op0=ALU.mult))
                xop(lambda e, c2=c2: e.scalar_tensor_tensor(out=wr[:, 0, :], in0=PW[:, 4, :], scalar=c2, in1=wr[:, 0, :], op0=ALU.mult, op1=ALU.add))
                xop(lambda e, c2=c2: e.scalar_tensor_tensor(out=wr[:, 1, :], in0=PW[:, 5, :], scalar=c2, in1=wr[:, 1, :], op0=ALU.mult, op1=ALU.add))
                xop(lambda e, c0=c0: e.tensor_scalar(out=wr[:, 0, :], in0=wr[:, 0, :], scalar1=c0, scalar2=None, op0=ALU.add))
                xop(lambda e, r=r: e.tensor_tensor(out=w1[:], in0=wr[:, 0, :], in1=Gs[:, r, 0, :], op=ALU.mult))
                xop(lambda e, r=r: e.tensor_tensor(out=w2[:], in0=wr[:, 1, :], in1=Gs[:, r, 1, :], op=ALU.mult))
                xop(lambda e: e.tensor_tensor(out=w1[:], in0=w1[:], in1=w2[:], op=ALU.subtract))
                xop(lambda e: e.tensor_tensor(out=car[:, 0, :], in0=car[:, 0, :], in1=w1[:], op=ALU.add))
                xop(lambda e, r=r: e.tensor_tensor(out=w1[:], in0=wr[:, 0, :], in1=Gs[:, r, 1, :], op=ALU.mult))
                xop(lambda e, r=r: e.tensor_tensor(out=w2[:], in0=wr[:, 1, :], in1=Gs[:, r, 0, :], op=ALU.mult))
                xop(lambda e: e.tensor_tensor(out=w1[:], in0=w1[:], in1=w2[:], op=ALU.add))
                xop(lambda e: e.tensor_tensor(out=car[:, 1, :], in0=car[:, 1, :], in1=w1[:], op=ALU.add))
            for ci in range(nchunks):
                chunk_states(ci)
                recur(NB, car[:, 0, :], car[:, 1, :], [Bcar])
                k.op("dve", lambda e: e.tensor_copy(out=car[:], in_=XH[:, NB, :, :]), reads=[Bxh, Bxhi], writes=[Bcar])
                chunk_outputs(ci)
            k.dma("sp", STo[j, 0], car[:], [Bcar], [B_out])
            chunk_states(nchunks)
            recur(8, stt[:, 0, :], stt[:, 1, :], [Bstt])
            k.op("dve", lambda e: e.tensor_copy(out=stt[:], in_=XH[:, 8, :, :]), reads=[Bxh, Bxhi], writes=[Bstt])
            chunk_outputs(nchunks)
            k.dma("sp", STo[j, 1], stt[:], [Bstt], [B_out])
            k.barrier()

    try:
        for li in CFG["layers"]:
            if li % 2 == 0:
                attn_layer(li, li // 2)
            else:
                ssm_layer(li, li // 2)
            mlp_layer(li)
    except Stop:
        pass
    if dbg is not None:
        k.dma("sp", dbg, XT[cur[0]], [B_XT[cur[0]]], [B_out])
    norm_phase(8, to_dram=yT)
    if any(li % 2 == 1 for li in CFG["layers"]):
        k.dma("sp", st_out, STo, [], [B("sto_out")])
    k.final_wait()


_NC_CACHE = {}


def _prep_inputs(inp):
    f = lambda a: np.ascontiguousarray(np.asarray(a, dtype=np.float32))
    x_prompt, x_sample = np.asarray(inp["x_prompt"]), np.asarray(inp["x_sample"])
    gam = np.concatenate([np.asarray(inp["norm_mix"]), np.asarray(inp["norm_mlp"]), np.asarray(inp["norm_final"])[None]], 0)
    gam = f(gam.reshape(9, FT, 128).transpose(2, 0, 1))
    bias = np.asarray(inp["attn_rel_bias"])
    kk = np.arange(128)[:, None, None]
    oo = np.arange(5)[None, :, None]
    qq = np.arange(128)[None, None, :]
    idx = np.minimum(128 * (5 - oo) - kk + qq, 256)
    ebraw = f(bias[:, :, idx])
    shared = {
        "gam": gam,
        "ebraw": ebraw,
        "s_are": f(inp["ssm_a_re"]), "s_aim": f(inp["ssm_a_im"]),
        "s_ldt": f(np.asarray(inp["ssm_log_dt"]).reshape(2, 128, 1)),
        "s_bre": f(np.asarray(inp["ssm_b_re"]).reshape(2, 128, 1024)),
        "s_bim": f(np.asarray(inp["ssm_b_im"]).reshape(2, 128, 1024)),
        "s_cre": f(np.asarray(inp["ssm_c_re"]).reshape(2, 128, 1024)),
        "s_cim": f(np.asarray(inp["ssm_c_im"]).reshape(2, 128, 1024)),
        "s_d": f(np.asarray(inp["ssm_d"]).reshape(2, FT, 128).transpose(0, 2, 1)),
    }
    ck, cvv = np.asarray(inp["cache_attn_k"]), np.asarray(inp["cache_attn_v"])
    sre, sim = np.asarray(inp["state_ssm_re"]), np.asarray(inp["state_ssm_im"])
    in_maps = []
    for c in range(NCORES):
        b, s = c // 4, c % 4
        xT = f(np.concatenate([x_prompt[b, s * NP:(s + 1) * NP], x_sample[c]], 0).T)
        sel8 = np.zeros((128, 8), np.float32)
        selc = np.zeros((128, 8, 3), np.float32)
        if s > 0:
            sel8[:, c - 1] = 1.0
        for r in range(b * 4, c):
            selc[:, r, c - 1 - r] = 1.0
        st = np.stack([sre[:, c], sim[:, c]], 1)
        st = st.reshape(2, 2, 2, 64, 64).transpose(0, 2, 4, 1, 3)
        m = dict(shared)
        for nm, key in (("wqkv", "attn_w_qkv"), ("wo", "attn_w_o"), ("wga", "ssm_w_glu_a"), ("wgb", "ssm_w_glu_b"),
                        ("wup", "mlp_w_up"), ("wdn", "mlp_w_down")):
            w2 = np.asarray(inp[key]).reshape(-1, np.asarray(inp[key]).shape[-1])
            r = w2.shape[0] // NCORES
            m[nm + "_sh"] = f(w2[c * r:(c + 1) * r])
        m.update({
            "xT": xT,
            "ckT": f(ck[:, c].reshape(2, 512, D).transpose(0, 2, 1)),
            "cv": f(cvv[:, c].reshape(2, 512, D)),
            "sel8": sel8, "hv": np.full((128, 1), 1.0 if s > 0 else 0.0, np.float32),
            "selc": f(selc.reshape(128, 24)),
            "st_in": f(st.reshape(2, 128, 2, 64)),
        })
        in_maps.append(m)
    return in_maps


def _assemble(res):
    y_prompt = np.zeros((2, 8192, D), np.float32)
    y_sample = np.zeros((8, 64, D), np.float32)
    nkp = np.zeros((2, 2, 512, 16, 128), np.float32); nvp = np.zeros_like(nkp)
    nks = np.zeros((2, 8, 64, 16, 128), np.float32); nvs = np.zeros_like(nks)
    srp = np.zeros((2, 2, 128, 64), np.float32); sip = np.zeros_like(srp)
    srs = np.zeros((2, 8, 128, 64), np.float32); sis = np.zeros_like(srs)
    for c in range(NCORES):
        r = res[c]
        b, s = c // 4, c % 4
        yT = r["yT"]
        y_prompt[b, s * NP:(s + 1) * NP] = yT[:, :NP].T
        y_sample[c] = yT[:, NP:].T
        kT, vT = r["kT_out"], r["vT_out"]
        nks[:, c] = kT[:, :, 512:].transpose(0, 2, 1).reshape(2, 64, 16, 128)
        nvs[:, c] = vT[:, :, 512:].transpose(0, 2, 1).reshape(2, 64, 16, 128)
        st = r["st_out"].reshape(2, 2, 2, 64, 2, 64)
        st = st.transpose(0, 1, 4, 2, 5, 3).reshape(2, 2, 2, 128, 64)
        srs[:, c], sis[:, c] = st[:, 1, 0], st[:, 1, 1]
        if s == 3:
            nkp[:, b] = kT[:, :, :512].transpose(0, 2, 1).reshape(2, 512, 16, 128)
            nvp[:, b] = vT[:, :, :512].transpose(0, 2, 1).reshape(2, 512, 16, 128)
            srp[:, b], sip[:, b] = st[:, 0, 0], st[:, 0, 1]
    return (y_prompt, y_sample, nkp, nvp, srp, sip, nks, nvs, srs, sis)


def kernel(**inputs):
    if "nc" not in _NC_CACHE:
        _NC_CACHE["nc"] = build_program()
    in_maps = _prep_inputs(inputs)
    res = run_bass_kernel_spmd(_NC_CACHE["nc"], in_maps, core_ids=list(range(NCORES)))
    return _assemble(res.results)
```
